# Optimizing a Trainium2 kernel written in Bass

```python
import math
import jax, jax.numpy as jnp
from jax import lax
import numpy as np

D_MODEL = 1024
BATCH = 8
SEQ = 4096
DEPTH = 2

CHUNK = 64
S5_WIDTH = 512
S5_GROUP = 16
S5_GROUPS = S5_WIDTH // S5_GROUP
S5_STATE = 64
CONV_WIDTH = 512
CONV_K = 3
ATTN_HEADS = 8
HEAD_DIM = 64
ATTN_WIDTH = ATTN_HEADS * HEAD_DIM
Q_BLOCK = 128
N_BRANCH = 3
MIX_WIDTH = S5_WIDTH + CONV_WIDTH + ATTN_WIDTH
D_FF = 4 * D_MODEL
IN_WIDTH = S5_WIDTH + 3 * CONV_WIDTH + 3 * ATTN_WIDTH + ATTN_HEADS + N_BRANCH * D_MODEL
EPS = 1e-6
NEG_INF = -1e30

kernel_name = "hybrid_s5_shortconv_fox_gated_encoder"


def rmsnorm(x, g):
    xf = x.astype(jnp.float32)
    y = xf * lax.rsqrt(jnp.mean(xf * xf, axis=-1, keepdims=True) + EPS)
    return (y * g.astype(jnp.float32)).astype(x.dtype)


def s5_mixer(u, a_re, a_im, log_dt, b_re, b_im, c_re, c_im, d_skip, w_glu):
    bsz, L, _ = u.shape
    f32 = jnp.float32
    uf = u.astype(f32).reshape(bsz, L, S5_GROUPS, S5_GROUP)
    lam = lax.complex(jnp.minimum(a_re.astype(f32), -1e-4), a_im.astype(f32))
    dt = jnp.exp(log_dt.astype(f32))[:, None]
    a_bar = jnp.exp(lam * dt)
    b = lax.complex(b_re.astype(f32), b_im.astype(f32))
    b_bar = ((a_bar - 1.0) / lam)[..., None] * b
    bu = lax.complex(jnp.einsum('gph,blgh->blgp', jnp.real(b_bar), uf),
                     jnp.einsum('gph,blgh->blgp', jnp.imag(b_bar), uf))
    a_seq = jnp.broadcast_to(a_bar, bu.shape)

    def combine(left, right):
        a_l, s_l = left
        a_r, s_r = right
        return a_r * a_l, a_r * s_l + s_r

    _, states = lax.associative_scan(combine, (a_seq, bu), axis=1)
    y = (jnp.einsum('ghp,blgp->blgh', c_re.astype(f32), jnp.real(states))
         - jnp.einsum('ghp,blgp->blgh', c_im.astype(f32), jnp.imag(states))
         + d_skip.astype(f32) * uf)
    y = jax.nn.gelu(y)
    y = y * jax.nn.sigmoid(jnp.einsum('blgh,ghk->blgk', y, w_glu.astype(f32)))
    return y.reshape(bsz, L, S5_WIDTH).astype(u.dtype)


def short_conv_mixer(x_in, gate_b, gate_c, conv_w):
    L = x_in.shape[1]
    v = gate_c * x_in
    vp = jnp.pad(v, ((0, 0), (CONV_K - 1, 0), (0, 0)))
    conv = vp[:, 0:L] * conv_w[0]
    for tap in range(1, CONV_K):
        conv = conv + vp[:, tap:tap + L] * conv_w[tap]
    return gate_b * conv


def forgetting_attention(q, k, v, f_logit, b_f):
    bsz, L, _ = q.shape
    q = q.reshape(bsz, L, ATTN_HEADS, HEAD_DIM)
    k = k.reshape(bsz, L, ATTN_HEADS, HEAD_DIM)
    v = v.reshape(bsz, L, ATTN_HEADS, HEAD_DIM)
    log_f = jax.nn.log_sigmoid((f_logit + b_f).astype(jnp.float32))
    cum = jnp.cumsum(log_f, axis=1).transpose(0, 2, 1)
    scale = HEAD_DIM ** -0.5
    outs = []
    for blk in range(L // Q_BLOCK):
        q0 = blk * Q_BLOCK
        kend = q0 + Q_BLOCK
        s = jnp.einsum('bqhd,bkhd->bhqk', q[:, q0:kend], k[:, :kend]).astype(jnp.float32) * scale
        s = s + cum[:, :, q0:kend, None] - cum[:, :, None, :kend]
        q_pos = q0 + jnp.arange(Q_BLOCK)
        k_pos = jnp.arange(kend)
        s = jnp.where(k_pos[None, :] <= q_pos[:, None], s, NEG_INF)
        p = jax.nn.softmax(s, axis=-1)
        outs.append(jnp.einsum('bhqk,bkhd->bqhd', p.astype(v.dtype), v[:, :kend]))
    return jnp.concatenate(outs, axis=1).reshape(bsz, L, ATTN_WIDTH)


def hybrid_layer(x, g_pre_mix, w_in, b_gate, a_re, a_im, log_dt, b_re, b_im, c_re, c_im,
                 d_skip, w_glu, conv_w, b_f, w_branch, w_out, g_post_mix, g_pre_mlp,
                 w_ff1, w_ff2, g_post_mlp):
    bsz, L, _ = x.shape
    h = rmsnorm(x, g_pre_mix)
    proj = jnp.einsum('bld,dn->bln', h, w_in)
    widths = [S5_WIDTH, CONV_WIDTH, CONV_WIDTH, CONV_WIDTH, ATTN_WIDTH, ATTN_WIDTH,
              ATTN_WIDTH, ATTN_HEADS, N_BRANCH * D_MODEL]
    offsets = np.cumsum(widths)[:-1].tolist()
    (u_s5, x_conv, b_conv, c_conv, q, k, v, f_logit, gate_logits) = jnp.split(proj, offsets, axis=-1)

    y_a = s5_mixer(u_s5, a_re, a_im, log_dt, b_re, b_im, c_re, c_im, d_skip, w_glu)
    y_b = short_conv_mixer(x_conv, b_conv, c_conv, conv_w)
    y_c = forgetting_attention(q, k, v, f_logit, b_f)

    gates = jax.nn.sigmoid((gate_logits + b_gate).astype(jnp.float32)).astype(x.dtype)
    gates = gates.reshape(bsz, L, N_BRANCH, D_MODEL)
    wa = w_branch[:S5_WIDTH]
    wb = w_branch[S5_WIDTH:S5_WIDTH + CONV_WIDTH]
    wc = w_branch[S5_WIDTH + CONV_WIDTH:]
    merged = (gates[:, :, 0] * jnp.einsum('blc,cd->bld', y_a, wa)
              + gates[:, :, 1] * jnp.einsum('blc,cd->bld', y_b, wb)
              + gates[:, :, 2] * jnp.einsum('blc,cd->bld', y_c, wc))
    mix = jnp.einsum('bld,de->ble', merged, w_out)
    x = x + rmsnorm(mix, g_post_mix)

    h = rmsnorm(x, g_pre_mlp)
    ff = jnp.einsum('blf,fd->bld', jnp.square(jax.nn.relu(jnp.einsum('bld,df->blf', h, w_ff1))), w_ff2)
    return x + rmsnorm(ff, g_post_mlp)


def setup_inputs(seed: int = 0) -> dict:
    key = jax.random.key(seed)
    ks = jax.random.split(key, 24)
    f32 = jnp.float32
    nrm = lambda k, shape, s: (jax.random.normal(k, shape, f32) * s)
    G, P, H = S5_GROUPS, S5_STATE, S5_GROUP
    x = jax.random.normal(ks[0], (BATCH, SEQ, D_MODEL), f32)
    g_pre_mix = 1.0 + nrm(ks[1], (DEPTH, D_MODEL), 0.05)
    w_in = nrm(ks[2], (DEPTH, D_MODEL, IN_WIDTH), D_MODEL ** -0.5)
    b_gate = nrm(ks[3], (DEPTH, N_BRANCH * D_MODEL), 0.01)
    n_idx = jnp.arange(P, dtype=f32)
    s5_a_re = -0.5 + nrm(ks[4], (DEPTH, G, P), 0.01)
    s5_a_im = math.pi * n_idx + nrm(ks[5], (DEPTH, G, P), 0.01)
    s5_log_dt = jax.random.uniform(ks[6], (DEPTH, G), f32, math.log(1e-3), math.log(1e-1))
    s5_b_re = nrm(ks[7], (DEPTH, G, P, H), (2.0 * H) ** -0.5)
    s5_b_im = nrm(ks[8], (DEPTH, G, P, H), (2.0 * H) ** -0.5)
    s5_c_re = nrm(ks[9], (DEPTH, G, H, P), (2.0 * P) ** -0.5)
    s5_c_im = nrm(ks[10], (DEPTH, G, H, P), (2.0 * P) ** -0.5)
    s5_d = nrm(ks[11], (DEPTH, G, H), 1.0)
    s5_w_glu = nrm(ks[12], (DEPTH, G, H, H), H ** -0.5)
    conv_w = nrm(ks[13], (DEPTH, CONV_K, CONV_WIDTH), CONV_K ** -0.5)
    fox_b_f = jnp.linspace(1.0, 6.0, ATTN_HEADS, dtype=f32)[None, :] + nrm(ks[14], (DEPTH, ATTN_HEADS), 0.1)
    w_branch = nrm(ks[15], (DEPTH, MIX_WIDTH, D_MODEL), S5_WIDTH ** -0.5)
    w_out = nrm(ks[16], (DEPTH, D_MODEL, D_MODEL), D_MODEL ** -0.5)
    g_post_mix = 1.0 + nrm(ks[17], (DEPTH, D_MODEL), 0.05)
    g_pre_mlp = 1.0 + nrm(ks[18], (DEPTH, D_MODEL), 0.05)
    w_ff1 = nrm(ks[19], (DEPTH, D_MODEL, D_FF), D_MODEL ** -0.5)
    w_ff2 = nrm(ks[20], (DEPTH, D_FF, D_MODEL), D_FF ** -0.5)
    g_post_mlp = 1.0 + nrm(ks[21], (DEPTH, D_MODEL), 0.05)
    return {"x": x, "g_pre_mix": g_pre_mix, "w_in": w_in, "b_gate": b_gate,
            "s5_a_re": s5_a_re, "s5_a_im": s5_a_im, "s5_log_dt": s5_log_dt,
            "s5_b_re": s5_b_re, "s5_b_im": s5_b_im, "s5_c_re": s5_c_re, "s5_c_im": s5_c_im,
            "s5_d": s5_d, "s5_w_glu": s5_w_glu, "conv_w": conv_w, "fox_b_f": fox_b_f,
            "w_branch": w_branch, "w_out": w_out, "g_post_mix": g_post_mix,
            "g_pre_mlp": g_pre_mlp, "w_ff1": w_ff1, "w_ff2": w_ff2, "g_post_mlp": g_post_mlp}


def reference(x, g_pre_mix, w_in, b_gate, s5_a_re, s5_a_im, s5_log_dt, s5_b_re, s5_b_im,
              s5_c_re, s5_c_im, s5_d, s5_w_glu, conv_w, fox_b_f, w_branch, w_out,
              g_post_mix, g_pre_mlp, w_ff1, w_ff2, g_post_mlp):
    for layer in range(DEPTH):
        x = hybrid_layer(x, g_pre_mix[layer], w_in[layer], b_gate[layer],
                         s5_a_re[layer], s5_a_im[layer], s5_log_dt[layer],
                         s5_b_re[layer], s5_b_im[layer], s5_c_re[layer], s5_c_im[layer],
                         s5_d[layer], s5_w_glu[layer], conv_w[layer], fox_b_f[layer],
                         w_branch[layer], w_out[layer], g_post_mix[layer], g_pre_mlp[layer],
                         w_ff1[layer], w_ff2[layer], g_post_mlp[layer])
    return x
```

```python
import contextlib
import math
import numpy as np
import concourse.bass as bass
import concourse.mybir as mybir
from concourse.bass_utils import run_bass_kernel_spmd

F32 = mybir.dt.float32
BF16 = mybir.dt.bfloat16
I32 = mybir.dt.int32
ALU = mybir.AluOpType
AF = mybir.ActivationFunctionType

D = 1024
DEPTH = 2
INW = 6664
EPS = 1e-6
ENGS = ("pe", "act", "dve", "pool", "sp")
NDSEM = 10
TWO_PI = 2.0 * math.pi

PNAMES = ["g_pre_mix", "w_in", "b_gate", "s5_a_re", "s5_a_im", "s5_log_dt", "s5_b_re", "s5_b_im",
          "s5_c_re", "s5_c_im", "s5_d", "s5_w_glu", "conv_w", "fox_b_f", "w_branch", "w_out",
          "g_post_mix", "g_pre_mlp", "w_ff1", "w_ff2", "g_post_mlp"]
PSHAPES = {"g_pre_mix": [D], "w_in": [D, INW], "b_gate": [3 * D], "s5_a_re": [32, 64], "s5_a_im": [32, 64],
           "s5_log_dt": [32], "s5_b_re": [32, 64, 16], "s5_b_im": [32, 64, 16], "s5_c_re": [32, 16, 64],
           "s5_c_im": [32, 16, 64], "s5_d": [32, 16], "s5_w_glu": [32, 16, 16], "conv_w": [3, 512],
           "fox_b_f": [8], "w_branch": [1536, D], "w_out": [D, D], "g_post_mix": [D], "g_pre_mlp": [D],
           "w_ff1": [D, 4 * D], "w_ff2": [4 * D, D], "g_post_mlp": [D]}


class _Op:
    __slots__ = ("eng", "idx", "fn", "deps", "dma", "dslot", "dval", "signal", "rank", "waits")

    def __init__(self, eng, idx, fn, dma):
        self.eng = eng
        self.idx = idx
        self.fn = fn
        self.dma = dma
        self.deps = []
        self.signal = False
        self.rank = 0
        self.waits = []
        self.dslot = None
        self.dval = 0


class Sch:
    def __init__(self, nc):
        self.nc = nc
        self.ops = {e: [] for e in ENGS}
        self.buf = {}
        self.ndma = {e: 0 for e in ENGS}
        self.dma_hist = {e: [] for e in ENGS}
        self.live_dma = []

    def op(self, eng, fn, reads=(), writes=(), dma=False):
        o = _Op(eng, len(self.ops[eng]), fn, dma)
        deps = []
        for k in reads:
            st = self.buf.get(k)
            if st is not None and st[0] is not None:
                deps.append(st[0])
        for k in writes:
            st = self.buf.get(k)
            if st is not None:
                if st[0] is not None:
                    deps.append(st[0])
                deps.extend(st[1])
        if dma:
            j = self.ndma[eng]
            self.ndma[eng] += 1
            o.dslot = j % NDSEM
            o.dval = 16 * (j // NDSEM + 1)
            if j >= NDSEM:
                deps.append(self.dma_hist[eng][j - NDSEM])
            self.dma_hist[eng].append(o)
            self.live_dma.append(o)
        o.deps = deps
        self.ops[eng].append(o)
        for k in reads:
            st = self.buf.setdefault(k, [None, []])
            st[1].append(o)
        for k in writes:
            self.buf[k] = [o, []]
        return o

    def barrier(self, keep_prefix=("wb",)):
        last = [self.ops[e][-1] for e in ENGS if self.ops[e]]
        dm = list(self.live_dma)
        self.live_dma = []
        for e in ENGS:
            o = self.op(e, None)
            o.deps = list(last) + dm

    def _resolve(self):
        for e in ENGS:
            seen = {f: -1 for f in ENGS}
            seen_dma = set()
            for o in self.ops[e]:
                need = {}
                for d in o.deps:
                    if d is o:
                        continue
                    if d.dma:
                        if id(d) in seen_dma:
                            continue
                        need[("d", id(d))] = d
                    else:
                        if d.fn is None:
                            continue
                        if d.eng == e and e == "pe":
                            continue
                        if d.idx <= seen[d.eng]:
                            continue
                        k = ("c", d.eng)
                        if k not in need or need[k].idx < d.idx:
                            need[k] = d
                o.waits = list(need.values())
                for d in o.waits:
                    d.signal = True
                    if d.dma:
                        seen_dma.add(id(d))
                    else:
                        seen[d.eng] = max(seen[d.eng], d.idx)
        for e in ENGS:
            r = 0
            for o in self.ops[e]:
                if o.dma:
                    continue
                if o.signal:
                    r += 1
                    o.rank = r

    def emit(self, final_waits=()):
        nc = self.nc
        fin = self.op("sp", None)
        fin.deps = list(final_waits)
        self._resolve()
        with contextlib.ExitStack() as st:
            csem = {e: st.enter_context(nc.semaphore("c_" + e)) for e in ENGS}
            dsem = {e: [st.enter_context(nc.semaphore("d_%s_%d" % (e, i))) for i in range(NDSEM)]
                    for e in ENGS if self.ndma[e] > 0}
            block = st.enter_context(nc.Block())

            def run(e, eng):
                for o in self.ops[e]:
                    for d in o.waits:
                        if d.dma:
                            eng.wait_ge(dsem[d.eng][d.dslot], d.dval)
                        else:
                            eng.wait_ge(csem[d.eng], d.rank)
                    if o.fn is None:
                        continue
                    ins = o.fn(eng)
                    if o.dma:
                        ins.then_inc(dsem[e][o.dslot], 16)
                    elif o.signal:
                        ins.then_inc(csem[e], 1)

            @block.tensor
            def _(eng):
                run("pe", eng)

            @block.scalar
            def _(eng):
                run("act", eng)

            @block.vector
            def _(eng):
                run("dve", eng)

            @block.gpsimd
            def _(eng):
                run("pool", eng)

            @block.sync
            def _(eng):
                run("sp", eng)


class _Stop(Exception):
    pass


def build(NT=8, layers=(0, 1), dbg=None):
    L = NT * 512
    NB = NT * 4
    nc = bass.Bass("TRN2", target_bir_lowering=False)
    x_in = nc.dram_tensor("x", [L, D], F32, kind="ExternalInput").ap()
    y_out = nc.dram_tensor("y", [L, D], F32, kind="ExternalOutput").ap()
    prm = {}
    for l in sorted(set(layers)):
        for n in PNAMES:
            prm[(n, l)] = nc.dram_tensor("%s_%d" % (n, l), PSHAPES[n], F32, kind="ExternalInput").ap()
    xmid = None
    if len(layers) > 1:
        xmid = [nc.dram_tensor("xmid%d" % i, [L, D], F32, kind="Internal").ap() for i in range(len(layers) - 1)]
    dbg_out = {}

    S = Sch(nc)
    ES = contextlib.ExitStack()
    _cnt = [0]

    def sbt(name, shape, dt):
        _cnt[0] += 1
        return nc.sbuf_tensor("%s_%d" % (name, _cnt[0]), shape, dt)

    def sb(name, shape, dt):
        return ES.enter_context(sbt(name, shape, dt))

    def P(eng, method, *args, r=(), w=(), **kw):
        return S.op(eng, lambda e: getattr(e, method)(*args, **kw), reads=r, writes=w)

    def DMA(q, out, in_, r=(), w=(), nonc=False):
        if nonc:
            def f(e):
                with nc.allow_non_contiguous_dma(reason="tiny parameter layout"):
                    return e.dma_start(out=out, in_=in_)
        else:
            def f(e):
                return e.dma_start(out=out, in_=in_)
        return S.op(q, f, reads=r, writes=w, dma=True)

    def stage(name, dumps):
        if dbg != name:
            return
        ods = []
        S.barrier()
        with contextlib.ExitStack() as dst_:
            cur = [0] * 8
            for i, ap in enumerate(dumps):
                pn, n = ap.shape[0], ap.shape[1]
                rb = i % 8
                c0 = cur[rb]
                cur[rb] += n
                tmp = dst_.enter_context(sbt("dbgtmp", [128, n], F32))
                P("dve", "tensor_copy", tmp[0:pn, 0:n], ap, w=[("dbg", i)])
                ods.append(DMA("sp", y_out[128 * rb:128 * rb + pn, c0:c0 + n], tmp[0:pn, 0:n], r=[("dbg", i)], nonc=True))
            S.barrier()
            S.emit(ods)
        raise _Stop()

    try:
      with ES:
          ident_bf = sb("ident_bf", [128, 128], BF16)
          ident_f = sb("ident_f", [128, 128], F32)
          maskT = sb("maskT", [128, 128], BF16)
          tri_f = sb("tri_f", [128, 128], F32)
          ones_f = sb("ones_f", [128, 128], F32)
          ones_bf = sb("ones_bf", [128, 128], BF16)
          iota_f = sb("iota_f", [128, 256], F32)
          KT = sb("KT", [128, 4, L], BF16)
          Vc = sb("Vc", [128, NB, 8, 65], BF16)
          ncum = sb("ncum", [128, NB, 8], F32)
          LB = sb("LB", [128, 16, 2, 128], BF16)
          CP = sb("CP", [128, 16, 2, 128], BF16)
          Rc = sb("Rc", [128, 16, 256], BF16)
          Rs = sb("Rs", [128, 16, 256], BF16)
          WG = sb("WG", [128, 4, 128], BF16)
          rr = sb("rr", [128, 16], F32)
          car_re = sb("car_re", [128, 16], F32)
          car_im = sb("car_im", [128, 16], F32)
          dT = sb("dT", [128, 4], F32)
          cw = sb("cw", [128, 4, 3], F32)
          halo = sb("halo", [128, 4, 2], F32)
          bg = sb("bg", [128, 24], F32)
          bf_rep = sb("bf_rep", [128, 8], F32)
          gpre = sb("gpre", [128, 2, 8], F32)
          gpost = sb("gpost", [128, D], F32)
          ncarry = sb("ncarry", [128, 8], F32)
          auxq = sb("auxq", [128, 8, 4], BF16)
          xt = sb("xt", [128, 4, D], F32)
          hT = sb("hT", [128, 8, 512], BF16)
          NWB = 3
          wbs = [sb("wb%d" % i, [128, 4096], BF16) for i in range(NWB)]
          NPS = 6
          pss = [ES.enter_context(nc.psum_tensor("ps%d" % i, [128, 512], F32)) for i in range(NPS)]
          psTs = [ES.enter_context(nc.psum_tensor("psT%d" % i, [128, 1024], BF16)) for i in range(2)]

          st = {"wb": 0, "ps": 0, "acc": 0}

          def newps():
              i = st["ps"] % 4
              st["ps"] += 1
              return pss[i], ("ps", i)

          def accps():
              i = 4 + st["acc"] % 2
              st["acc"] += 1
              return pss[i], ("ps", i)

          def load_w(src_ap, view):
              i = st["wb"] % NWB
              st["wb"] += 1
              dst = view(wbs[i])
              DMA("pool", dst, src_ap, w=[("wb", i)])
              return wbs[i], ("wb", i)

          P("pool", "memset", ident_f[:], 0.0, w=["ident_f"])
          P("pool", "affine_select", out=ident_f[:], in_=ident_f[:], pattern=[[-1, 128]], compare_op=ALU.not_equal,
            fill=1.0, base=0, channel_multiplier=1, r=["ident_f"], w=["ident_f"])
          P("dve", "tensor_copy", ident_bf[:], ident_f[:], r=["ident_f"], w=["ident_bf"])
          P("pool", "memset", tri_f[:], 1.0, w=["tri_f"])
          P("pool", "affine_select", out=tri_f[:], in_=tri_f[:], pattern=[[1, 128]], compare_op=ALU.is_ge,
            fill=0.0, base=0, channel_multiplier=-1, r=["tri_f"], w=["tri_f"])
          P("dve", "tensor_scalar", maskT[:], tri_f[:], -1.0, 30000.0, ALU.add, ALU.mult, r=["tri_f"], w=["maskT"])
          P("dve", "memset", ones_f[:], 1.0, w=["ones_f"])
          P("dve", "memset", ones_bf[:], 1.0, w=["ones_bf"])
          P("dve", "memset", Vc[:], 1.0, w=["Vc_all"])
          with sbt("iota_i", [128, 256], I32) as iota_i:
              P("pool", "iota", iota_i[:], pattern=[[1, 256]], base=1, channel_multiplier=0, w=["iota_i"])
              P("dve", "tensor_copy", iota_f[:], iota_i[:], r=["iota_i"], w=["iota_f"])
              S.barrier()

          for li, l in enumerate(layers):
              xsrc = x_in if li == 0 else xmid[li - 1]
              xdst = y_out if li == len(layers) - 1 else xmid[li]
              pr = lambda n: prm[(n, l)]

              setup = contextlib.ExitStack()
              with setup:
                  def tb(name, shape, dt=F32):
                      return setup.enter_context(sbt(name, shape, dt))
                  are = tb("are", [128, 16]); aim = tb("aim", [128, 16]); ldt = tb("ldt", [128, 16])
                  bre = tb("bre", [128, 16, 16]); bim = tb("bim", [128, 16, 16])
                  cdr = tb("cdr", [128, 4, 2, 64]); cdi = tb("cdi", [128, 4, 2, 64])
                  wgf = tb("wgf", [128, 4, 128])
                  smask = tb("smask", [128, 4, 128])
                  zall = tb("zall", [128, 8, 2, 128])
                  t16 = [tb("t16_%d" % i, [128, 16]) for i in range(14)]
                  bbr = tb("bbr", [128, 16, 16]); bbi = tb("bbi", [128, 16, 16]); tb3 = tb("tb3", [128, 16, 16])
                  ang = tb("ang", [128, 256]); ang2 = tb("ang2", [128, 256]); angi = tb("angi", [128, 256], I32); m1i = tb("m1i", [128, 16], I32)

                  DMA("sp", are[:], pr("s5_a_re").rearrange("g p -> (g p)").rearrange("(t k) -> k t", k=128), w=["are"], nonc=True)
                  DMA("sp", aim[:], pr("s5_a_im").rearrange("g p -> (g p)").rearrange("(t k) -> k t", k=128), w=["aim"], nonc=True)
                  ldsrc = pr("s5_log_dt").rearrange("(t gg) -> gg t", gg=2)
                  for gg in range(2):
                      DMA("sp", ldt[64 * gg:64 * gg + 64, :], ldsrc[gg:gg + 1, :].partition_broadcast(64), w=["ldt%d" % gg], nonc=True)
                      DMA("sp", bre[64 * gg:64 * gg + 64, :, :], pr("s5_b_re").rearrange("(t gg) p h -> gg p t h", gg=2)[gg], w=["bre%d" % gg], nonc=True)
                      DMA("sp", bim[64 * gg:64 * gg + 64, :, :], pr("s5_b_im").rearrange("(t gg) p h -> gg p t h", gg=2)[gg], w=["bim%d" % gg], nonc=True)
                  for dup in range(2):
                      DMA("sp", cdr[:, :, dup, :], pr("s5_c_re").rearrange("(q g8) h p -> (g8 h) q p", g8=8), w=["cdr%d" % dup])
                      DMA("sp", cdi[:, :, dup, :], pr("s5_c_im").rearrange("(q g8) h p -> (g8 h) q p", g8=8), w=["cdi%d" % dup])
                  DMA("sp", dT[:], pr("s5_d").rearrange("g h -> (g h)").rearrange("(q k) -> k q", k=128), w=["dT"], nonc=True)
                  for tap in range(3):
                      DMA("sp", cw[:, :, tap], pr("conv_w")[tap].rearrange("(m k) -> k m", k=128), w=["cw%d" % tap], nonc=True)
                  DMA("sp", bg[:], pr("b_gate").rearrange("(c k) -> k c", k=128), w=["bg"], nonc=True)
                  DMA("sp", bf_rep[:], pr("fox_b_f").rearrange("(o h) -> o h", o=1).partition_broadcast(128), w=["bf_rep"], nonc=True)
                  DMA("sp", gpre[:, 0, :], pr("g_pre_mix").rearrange("(c k) -> k c", k=128), w=["gpre0"], nonc=True)
                  DMA("sp", gpre[:, 1, :], pr("g_pre_mlp").rearrange("(c k) -> k c", k=128), w=["gpre1"], nonc=True)
                  P("dve", "memset", wgf[:], 0.0, w=["wgf"])
                  wgsrc = pr("s5_w_glu").rearrange("(q g8) h k -> g8 h q k", g8=8)
                  for g8 in range(8):
                      DMA("sp", wgf[16 * g8:16 * g8 + 16, :, 16 * g8:16 * g8 + 16], wgsrc[g8], r=["wgf"], w=["wgf%d" % g8], nonc=True)
                  P("dve", "tensor_copy", WG[:], wgf[:], r=["wgf"] + ["wgf%d" % g for g in range(8)], w=["WG"])
                  P("dve", "memset", smask[:], 0.0, w=["smask"])
                  for tq in range(4):
                      for gg in range(2):
                          P("dve", "memset", smask[64 * gg:64 * gg + 64, tq, 32 * tq + 16 * gg:32 * tq + 16 * gg + 16], 1.0,
                            r=["smask"], w=["smask"])
                  P("dve", "memset", car_re[:], 0.0, w=["car_re"])
                  P("dve", "memset", car_im[:], 0.0, w=["car_im"])
                  P("dve", "memset", halo[:], 0.0, w=["halo"])
                  P("dve", "memset", ncarry[:], 0.0, w=["ncarry"])
                  ldk = ["ldt0", "ldt1"]
                  (dtt, lr, th, m1, sn, cs, abr, abi, den, kr, ki, tA, tB, tC) = t16
                  P("act", "activation", dtt[:], ldt[:], AF.Exp, r=ldk, w=["dtt"])
                  P("dve", "tensor_scalar", lr[:], are[:], -1e-4, None, ALU.min, r=["are"], w=["lr"])
                  P("dve", "tensor_tensor", tA[:], lr[:], dtt[:], ALU.mult, r=["lr", "dtt"], w=["tA"])
                  P("act", "activation", rr[:], tA[:], AF.Exp, r=["tA"], w=["rr"])
                  P("dve", "tensor_tensor", th[:], aim[:], dtt[:], ALU.mult, r=["aim", "dtt"], w=["th"])

                  def sincos(dst, src, shift, rk, wk, tmp, tmpi):
                      P("dve", "tensor_scalar", tmp, src, shift, 1.0 / TWO_PI, ALU.add, ALU.mult, r=rk, w=["sc_tmp"])
                      P("dve", "tensor_copy", tmpi, tmp, r=["sc_tmp"], w=["sc_tmpi"])
                      P("dve", "tensor_copy", tmp, tmpi, r=["sc_tmpi"], w=["sc_tmp"])
                      P("dve", "scalar_tensor_tensor", tmp, tmp, -TWO_PI, src, ALU.mult, ALU.add, r=["sc_tmp"] + rk, w=["sc_tmp"])
                      P("dve", "tensor_scalar", tmp, tmp, shift, math.pi, ALU.add, ALU.min, r=["sc_tmp"], w=["sc_tmp"])
                      P("dve", "tensor_scalar", tmp, tmp, -math.pi, None, ALU.max, r=["sc_tmp"], w=["sc_tmp"])
                      P("act", "activation", dst, tmp, AF.Sin, r=["sc_tmp"], w=wk)

                  sincos(sn[:], th[:], 0.0, ["th"], ["sn"], m1[:], m1i[:])
                  sincos(cs[:], th[:], math.pi / 2, ["th"], ["cs"], m1[:], m1i[:])
                  P("dve", "tensor_tensor", abr[:], rr[:], cs[:], ALU.mult, r=["rr", "cs"], w=["abr"])
                  P("dve", "tensor_tensor", abi[:], rr[:], sn[:], ALU.mult, r=["rr", "sn"], w=["abi"])
                  P("dve", "tensor_scalar", abr[:], abr[:], -1.0, None, ALU.add, r=["abr"], w=["abr"])
                  P("dve", "tensor_tensor", den[:], lr[:], lr[:], ALU.mult, r=["lr"], w=["den"])
                  P("dve", "tensor_tensor", tA[:], aim[:], aim[:], ALU.mult, r=["aim"], w=["tA"])
                  P("dve", "tensor_tensor", den[:], den[:], tA[:], ALU.add, r=["den", "tA"], w=["den"])
                  P("dve", "reciprocal", den[:], den[:], r=["den"], w=["den"])
                  P("dve", "tensor_tensor", tA[:], abr[:], lr[:], ALU.mult, r=["abr", "lr"], w=["tA"])
                  P("dve", "tensor_tensor", tB[:], abi[:], aim[:], ALU.mult, r=["abi", "aim"], w=["tB"])
                  P("dve", "tensor_tensor", tA[:], tA[:], tB[:], ALU.add, r=["tA", "tB"], w=["tA"])
                  P("dve", "tensor_tensor", kr[:], tA[:], den[:], ALU.mult, r=["tA", "den"], w=["kr"])
                  P("dve", "tensor_tensor", tA[:], abi[:], lr[:], ALU.mult, r=["abi", "lr"], w=["tA"])
                  P("dve", "tensor_tensor", tB[:], abr[:], aim[:], ALU.mult, r=["abr", "aim"], w=["tB"])
                  P("dve", "tensor_tensor", tA[:], tA[:], tB[:], ALU.subtract, r=["tA", "tB"], w=["tA"])
                  P("dve", "tensor_tensor", ki[:], tA[:], den[:], ALU.mult, r=["tA", "den"], w=["ki"])
                  krb = kr[:].unsqueeze(2).to_broadcast([128, 16, 16])
                  kib = ki[:].unsqueeze(2).to_broadcast([128, 16, 16])
                  bk = ["bre0", "bre1", "bim0", "bim1"]
                  P("dve", "tensor_tensor", bbr[:], bre[:], krb, ALU.mult, r=bk + ["kr"], w=["bbr"])
                  P("dve", "tensor_tensor", tb3[:], bim[:], kib, ALU.mult, r=bk + ["ki"], w=["tb3"])
                  P("dve", "tensor_tensor", bbr[:], bbr[:], tb3[:], ALU.subtract, r=["bbr", "tb3"], w=["bbr"])
                  P("dve", "tensor_tensor", bbi[:], bim[:], krb, ALU.mult, r=bk + ["kr"], w=["bbi"])
                  P("dve", "tensor_tensor", tb3[:], bre[:], kib, ALU.mult, r=bk + ["ki"], w=["tb3"])
                  P("dve", "tensor_tensor", bbi[:], bbi[:], tb3[:], ALU.add, r=["bbi", "tb3"], w=["bbi"])
                  for zh in range(2):
                      P("dve", "memset", zall[:], 0.0, r=["zall"], w=["zall"])
                      for tq in range(4):
                          for gg in range(2):
                              for ri, srcb in enumerate((bbr, bbi)):
                                  P("dve", "tensor_copy",
                                    zall[64 * gg:64 * gg + 64, tq::4, ri, 32 * tq + 16 * gg:32 * tq + 16 * gg + 16],
                                    srcb[64 * gg:64 * gg + 64, 8 * zh + tq:8 * zh + 8:4, :], r=["zall", "bbr", "bbi"], w=["zall"])
                      for t in range(8 * zh, 8 * zh + 8):
                          for ri in range(2):
                              ps, pk = newps()
                              P("pe", "transpose", ps[:, 0:128], zall[:, t - 8 * zh, ri, :], ident_f[:], r=["zall", "ident_f"], w=[pk])
                              P("act", "copy", LB[:, t, ri, :], ps[:, 0:128], r=[pk], w=["LB"])
                  for q in range(4):
                      for ri, cd in enumerate((cdr, cdi)):
                          ps, pk = newps()
                          P("pe", "transpose", ps[:, 0:128], cd[:, q, :, :].rearrange("p a b -> p (a b)"), ident_f[:],
                            r=["cdr0", "cdr1", "cdi0", "cdi1", "ident_f"], w=[pk])
                          for tq in range(4):
                              t = 4 * q + tq
                              P("dve", "scalar_tensor_tensor", CP[:, t, ri, :], ps[:, 0:128], (1.0 if ri == 0 else -1.0),
                                smask[:, tq, :], ALU.mult, ALU.mult, r=[pk, "smask"], w=["CP"])
                  for t in range(16):
                      P("dve", "tensor_scalar", ang[:], iota_f[:], th[:, t:t + 1], None, ALU.mult, r=["iota_f", "th"], w=["ang"])
                      sincos(Rs[:, t, :], ang[:], 0.0, ["ang"], ["Rs"], ang2[:], angi[:])
                      sincos(Rc[:, t, :], ang[:], math.pi / 2, ["ang"], ["Rc"], ang2[:], angi[:])
                  S.barrier()
                  stage("setup", [LB[:, 0, 0, :], LB[:, 5, 1, :], CP[:, 0, 0, :], CP[:, 7, 1, :], Rc[:, 3, :], Rs[:, 3, :], rr[:], WG[:, 1, :],
                                  sn[:], cs[:], th[:], kr[:], ki[:], dT[:], bg[:], gpre[:, 0, :], cw[:, :, 0], bf_rep[:]])

              for tt in range(NT):
                  T0 = 512 * tt
                  xk = [("xt", s) for s in range(4)]
                  DMA("sp", xt[:], xsrc[T0:T0 + 512, :].rearrange("(s p) d -> p s d", p=128), w=xk)

                  def norm_T(gidx, tagw):
                      with contextlib.ExitStack() as ph:
                          xn = ph.enter_context(sbt("xn", [128, 4, D], BF16))
                          junk = ph.enter_context(sbt("junk", [128, D], BF16))
                          ss = ph.enter_context(sbt("ss", [128, 4], F32))
                          rstd = ph.enter_context(sbt("rstd", [128, 4], F32))
                          P("dve", "memset", ss[:], 0.0, w=["ss"])
                          for s in range(4):
                              P("act", "activation", junk[:], xt[:, s, :], AF.Square, accum_out=ss[:, s:s + 1],
                                r=[("xt", s), "ss"], w=["junk", "ss"])
                          P("act", "activation", rstd[:], ss[:], AF.Sqrt, bias=EPS, scale=1.0 / D, r=["ss"], w=["rstd"])
                          P("dve", "reciprocal", rstd[:], rstd[:], r=["rstd"], w=["rstd"])
                          if gidx == 0:
                              stage("A0", [rstd[:], ss[:], xt[:, 0, :]])
                          for s in range(4):
                              P("act", "mul", xn[:, s, :], xt[:, s, :], rstd[:, s:s + 1],
                                r=[("xt", s), "rstd"], w=[("xn", s)])
                          if gidx == 0:
                              stage("A1", [xn[:, 0, :], xn[:, 3, :]])
                          for c in range(8):
                              half = c % 2
                              for s in range(4):
                                  P("pe", "transpose", psTs[half][:, 128 * s:128 * s + 128],
                                    xn[:, s, 128 * c:128 * c + 128], ident_bf[:],
                                    r=[("xn", s), "ident_bf"], w=[("psT", half)])
                              eng = "dve" if c % 2 == 0 else "act"
                              if eng == "dve":
                                  P("dve", "tensor_scalar", hT[:, c, :], psTs[half][:, 0:512],
                                    gpre[:, gidx, c:c + 1], None, ALU.mult, r=[("psT", half), "gpre%d" % gidx], w=[("hT", c)])
                              else:
                                  P("act", "mul", hT[:, c, :], psTs[half][:, 0:512],
                                    gpre[:, gidx, c:c + 1], r=[("psT", half), "gpre%d" % gidx], w=[("hT", c)])
                          S.barrier()

                  hk = [("hT", c) for c in range(8)]
                  norm_T(0, "a")
                  stage("A", [hT[:, 0, :], hT[:, 7, :]])

                  def inproj_block(c0, ncols=512):
                      wb, wk = load_w(prm[("w_in", l)][:, c0:c0 + ncols].rearrange("(kc p) n -> p kc n", p=128),
                                      lambda b: b[:, 0:8 * ncols].rearrange("p (kc n) -> p kc n", kc=8))
                      return wb[:, 0:8 * ncols].rearrange("p (kc n) -> p kc n", kc=8), wk

                  def fm_matmul(wv, wk, m, rhs_keys=hk):
                      ps, pk = newps()
                      for kc in range(8):
                          P("pe", "matmul", ps[:], wv[:, kc, 128 * m:128 * m + 128], hT[:, kc, :], start=(kc == 0), stop=(kc == 7),
                            r=[wk] + rhs_keys, w=[pk])
                      return ps, pk

                  mixer = contextlib.ExitStack()
                  ya = mixer.enter_context(sbt("ya", [128, 4, 512], BF16))
                  yb = mixer.enter_context(sbt("yb", [128, 4, 512], BF16))
                  yc = mixer.enter_context(sbt("yc", [64, 8, 512], BF16))
                  mrg = mixer.enter_context(sbt("mrg", [128, 8, 512], BF16))
                  with contextlib.ExitStack() as ph:
                      def tbuf(name, shape, dt=F32):
                          return ph.enter_context(sbt(name, shape, dt))
                      u_bf = tbuf("u_bf", [128, 4, 512], BF16)
                      zre = tbuf("zre", [128, 512]); zim = tbuf("zim", [128, 512])
                      tm1 = tbuf("tm1", [128, 512]); tm2 = tbuf("tm2", [128, 512])
                      wre = tbuf("wre", [128, 512]); wim = tbuf("wim", [128, 512])
                      sre = [tbuf("sre%d" % i, [128, 512], BF16) for i in range(2)]
                      sim = [tbuf("sim%d" % i, [128, 512], BF16) for i in range(2)]
                      ygb = tbuf("ygb", [128, 512], BF16)
                      c1 = tbuf("c1", [128, 4])
                      wv, wk = inproj_block(0)
                      for m in range(4):
                          ps, pk = fm_matmul(wv, wk, m)
                          P("act", "copy", u_bf[:, m, :], ps[:], r=[pk], w=[("u", m)])

                      def v3(ap):
                          return ap.rearrange("p (k i) -> p k i", k=2)
                      for q in range(4):
                          Yq, yk = accps()
                          for tq in range(4):
                              t = 4 * q + tq
                              pb = t % 2
                              p_r, prk = newps()
                              p_i, pik = newps()
                              P("pe", "matmul", p_r[:], LB[:, t, 0, :], u_bf[:, q, :], start=True, stop=True, r=["LB", ("u", q)], w=[prk])
                              P("pe", "matmul", p_i[:], LB[:, t, 1, :], u_bf[:, q, :], start=True, stop=True, r=["LB", ("u", q)], w=[pik])
                              Rc3 = Rc[:, t, :].unsqueeze(1).to_broadcast([128, 2, 256])
                              Rs3 = Rs[:, t, :].unsqueeze(1).to_broadcast([128, 2, 256])
                              P("dve", "tensor_tensor", v3(zre[:]), v3(p_r[:]), Rc3, ALU.mult, r=[prk, "Rc"], w=["zre"])
                              P("dve", "tensor_tensor", v3(tm1[:]), v3(p_i[:]), Rs3, ALU.mult, r=[pik, "Rs"], w=["tm1"])
                              P("dve", "tensor_tensor", zre[:], zre[:], tm1[:], ALU.add, r=["zre", "tm1"], w=["zre"])
                              P("dve", "tensor_tensor", v3(zim[:]), v3(p_i[:]), Rc3, ALU.mult, r=[pik, "Rc"], w=["zim"])
                              P("dve", "tensor_tensor", v3(tm2[:]), v3(p_r[:]), Rs3, ALU.mult, r=[prk, "Rs"], w=["tm2"])
                              P("dve", "tensor_tensor", zim[:], zim[:], tm2[:], ALU.subtract, r=["zim", "tm2"], w=["zim"])
                              rb = rr[:, t:t + 1].to_broadcast([128, 256])
                              for k in range(2):
                                  sl = slice(256 * k, 256 * k + 256)
                                  P("dve", "tensor_tensor_scan", wre[:, sl], rb, zre[:, sl], car_re[:, t:t + 1], ALU.mult, ALU.add,
                                    r=["rr", "zre", "car_re"], w=["wre"])
                                  P("dve", "tensor_tensor_scan", wim[:, sl], rb, zim[:, sl], car_im[:, t:t + 1], ALU.mult, ALU.add,
                                    r=["rr", "zim", "car_im"], w=["wim"])
                                  e = 256 * k + 255
                                  cC = Rc[:, t, 255:256]
                                  cS = Rs[:, t, 255:256]
                                  P("dve", "tensor_tensor", c1[:, 0:1], wim[:, e:e + 1], cS, ALU.mult, r=["wim", "Rs"], w=["c1"])
                                  P("dve", "tensor_tensor", c1[:, 1:2], wim[:, e:e + 1], cC, ALU.mult, r=["wim", "Rc"], w=["c1"])
                                  P("dve", "scalar_tensor_tensor", car_re[:, t:t + 1], wre[:, e:e + 1], cC, c1[:, 0:1], ALU.mult, ALU.subtract,
                                    r=["wre", "Rc", "c1"], w=["car_re"])
                                  P("dve", "scalar_tensor_tensor", car_im[:, t:t + 1], wre[:, e:e + 1], cS, c1[:, 1:2], ALU.mult, ALU.add,
                                    r=["wre", "Rs", "c1"], w=["car_im"])
                              P("dve", "tensor_tensor", v3(tm1[:]), v3(wre[:]), Rc3, ALU.mult, r=["wre", "Rc"], w=["tm1"])
                              P("dve", "tensor_tensor", v3(tm2[:]), v3(wim[:]), Rs3, ALU.mult, r=["wim", "Rs"], w=["tm2"])
                              P("dve", "tensor_tensor", sre[pb][:], tm1[:], tm2[:], ALU.subtract, r=["tm1", "tm2"], w=[("sre", pb)])
                              P("dve", "tensor_tensor", v3(tm1[:]), v3(wre[:]), Rs3, ALU.mult, r=["wre", "Rs"], w=["tm1"])
                              P("dve", "tensor_tensor", v3(tm2[:]), v3(wim[:]), Rc3, ALU.mult, r=["wim", "Rc"], w=["tm2"])
                              P("dve", "tensor_tensor", sim[pb][:], tm1[:], tm2[:], ALU.add, r=["tm1", "tm2"], w=[("sim", pb)])
                              P("pe", "matmul", Yq[:], CP[:, t, 0, :], sre[pb][:], start=(tq == 0), stop=False, r=["CP", ("sre", pb)], w=[yk])
                              P("pe", "matmul", Yq[:], CP[:, t, 1, :], sim[pb][:], start=False, stop=(tq == 3), r=["CP", ("sim", pb)], w=[yk])
                          P("dve", "scalar_tensor_tensor", zre[:], u_bf[:, q, :], dT[:, q:q + 1], Yq[:], ALU.mult, ALU.add,
                            r=[("u", q), "dT", yk], w=["zre"])
                          P("dve", "tensor_tensor", tm1[:], zre[:], zre[:], ALU.mult, r=["zre"], w=["tm1"])
                          P("dve", "tensor_scalar", tm1[:], tm1[:], 0.044715, 1.0, ALU.mult, ALU.add, r=["tm1"], w=["tm1"])
                          P("dve", "tensor_tensor", tm1[:], tm1[:], zre[:], ALU.mult, r=["tm1", "zre"], w=["tm1"])
                          P("act", "activation", tm2[:], tm1[:], AF.Sigmoid, scale=2.0 * math.sqrt(2.0 / math.pi), r=["tm1"], w=["tm2"])
                          P("dve", "tensor_tensor", zim[:], zre[:], tm2[:], ALU.mult, r=["zre", "tm2"], w=["zim"])
                          P("act", "copy", ygb[:], zim[:], r=["zim"], w=["ygb"])
                          G, gk = newps()
                          P("pe", "matmul", G[:], WG[:, q, :], ygb[:], start=True, stop=True, r=["WG", "ygb"], w=[gk])
                          P("act", "activation", tm2[:], G[:], AF.Sigmoid, r=[gk], w=["tm2"])
                          P("dve", "tensor_tensor", ya[:, q, :], zim[:], tm2[:], ALU.mult, r=["zim", "tm2"], w=[("ya", q)])
                      S.barrier()
                      stage("s5", [ya[:, q, :] for q in range(4)] + [u_bf[:, q, :] for q in range(4)])

                  with contextlib.ExitStack() as ph:
                      xc = ph.enter_context(sbt("xc", [128, 4, 512], F32))
                      vb = ph.enter_context(sbt("vb", [128, 4, 514], F32))
                      P("dve", "tensor_copy", vb[:, :, 0:2], halo[:], r=["halo"], w=["vb_h"])
                      wv, wk = inproj_block(512)
                      for m in range(4):
                          ps, pk = fm_matmul(wv, wk, m)
                          P("act", "copy", xc[:, m, :], ps[:], r=[pk], w=[("xc", m)])
                      wv, wk = inproj_block(1536)
                      for m in range(4):
                          ps, pk = fm_matmul(wv, wk, m)
                          P("dve", "tensor_tensor", vb[:, m, 2:514], ps[:], xc[:, m, :], ALU.mult, r=[pk, ("xc", m)], w=[("vb", m)])
                          P("dve", "tensor_scalar", xc[:, m, :], vb[:, m, 2:514], cw[:, m, 2:3], None, ALU.mult,
                            r=[("vb", m), "cw0", "cw1", "cw2"], w=[("xc", m)])
                          P("dve", "scalar_tensor_tensor", xc[:, m, :], vb[:, m, 1:513], cw[:, m, 1:2], xc[:, m, :], ALU.mult, ALU.add,
                            r=[("vb", m), "vb_h", "cw0", "cw1", "cw2", ("xc", m)], w=[("xc", m)])
                          P("dve", "scalar_tensor_tensor", xc[:, m, :], vb[:, m, 0:512], cw[:, m, 0:1], xc[:, m, :], ALU.mult, ALU.add,
                            r=[("vb", m), "vb_h", "cw0", "cw1", "cw2", ("xc", m)], w=[("xc", m)])
                      P("dve", "tensor_copy", halo[:], vb[:, :, 512:514], r=[("vb", m) for m in range(4)], w=["halo"])
                      wv, wk = inproj_block(1024)
                      for m in range(4):
                          ps, pk = fm_matmul(wv, wk, m)
                          P("dve", "tensor_tensor", yb[:, m, :], ps[:], xc[:, m, :], ALU.mult, r=[pk, ("xc", m)], w=[("yb", m)])
                      S.barrier()
                      stage("conv", [yb[:, m, :] for m in range(4)])

                  with contextlib.ExitStack() as ph:
                      def tbuf(name, shape, dt=F32):
                          return ph.enter_context(sbt(name, shape, dt))
                      qT = tbuf("qT", [128, 4, 512], BF16)
                      pT = [tbuf("pT%d" % i, [128, 512], BF16) for i in range(3)]
                      wf = tbuf("wf", [128, 8, 8], BF16)
                      fl = tbuf("fl", [128, 8]); lsp = tbuf("lsp", [128, 8]); cq = tbuf("cq", [128, 8])
                      r1 = tbuf("r1", [128, 8])
                      cb = [tbuf("cb%d" % i, [128, 8], BF16) for i in range(3)]
                      onum = tbuf("onum", [64, 512])
                      rden = tbuf("rden", [128, 512])
                      DMA("pool", wf[:], prm[("w_in", l)][:, 3584:3592].rearrange("(kc p) n -> p kc n", p=128), w=["wf"], nonc=True)
                      wv, wk = inproj_block(2048)
                      for m in range(4):
                          ps, pk = fm_matmul(wv, wk, m)
                          P("act", "mul", qT[:, m, :], ps[:], 0.125, r=[pk], w=[("qT", m)])
                      wv, wk = inproj_block(2560)
                      for m in range(4):
                          ps, pk = fm_matmul(wv, wk, m)
                          P("act", "copy", KT[:, m, T0:T0 + 512], ps[:], r=[pk], w=[("KT", m)])
                      wv, wk = inproj_block(3072)
                      for s in range(4):
                          ps, pk = newps()
                          for kc in range(8):
                              P("pe", "matmul", ps[:], hT[:, kc, 128 * s:128 * s + 128], wv[:, kc, :], start=(kc == 0), stop=(kc == 7),
                                r=[wk] + hk, w=[pk])
                          P("act", "copy", Vc[:, 4 * tt + s, :, 0:64], ps[:].rearrange("p (h d) -> p h d", h=8), r=[pk, "Vc_all"], w=["Vc"])
                      for s in range(4):
                          b = 4 * tt + s
                          ps, pk = newps()
                          for kc in range(8):
                              P("pe", "matmul", ps[:, 0:8], hT[:, kc, 128 * s:128 * s + 128], wf[:, kc, :], start=(kc == 0), stop=(kc == 7),
                                r=["wf"] + hk, w=[pk])
                          P("dve", "tensor_tensor", fl[:], ps[:, 0:8], bf_rep[:], ALU.add, r=[pk, "bf_rep"], w=["fl"])
                          P("act", "activation", fl[:], fl[:], AF.Exp, scale=-1.0, r=["fl"], w=["fl"])
                          P("act", "activation", lsp[:], fl[:], AF.Ln, bias=1.0, r=["fl"], w=["lsp"])
                          ps2, pk2 = newps()
                          P("pe", "matmul", ps2[:, 0:8], tri_f[:], lsp[:], start=True, stop=True, r=["tri_f", "lsp"], w=[pk2])
                          P("pe", "matmul", ps2[:, 8:16], ones_f[:], lsp[:], start=True, stop=True, r=["ones_f", "lsp"], w=[pk2])
                          P("dve", "tensor_tensor", ncum[:, b, :], ps2[:, 0:8], ncarry[:], ALU.add, r=[pk2, "ncarry"], w=["ncum"])
                          P("dve", "scalar_tensor_tensor", cq[:], ps2[:, 8:16], 0.5, ncarry[:], ALU.mult, ALU.add, r=[pk2, "ncarry"], w=["cq"])
                          P("dve", "tensor_scalar", cq[:], cq[:], -1.0, None, ALU.mult, r=["cq"], w=["cq"])
                          P("dve", "tensor_tensor", ncarry[:], ncarry[:], ps2[:, 8:16], ALU.add, r=[pk2, "ncarry"], w=["ncarry"])
                          P("dve", "tensor_copy", cb[0][:], cq[:], r=["cq"], w=["cb0"])
                          P("dve", "tensor_tensor", r1[:], cq[:], cb[0][:], ALU.subtract, r=["cq", "cb0"], w=["r1"])
                          P("dve", "tensor_copy", cb[1][:], r1[:], r=["r1"], w=["cb1"])
                          P("dve", "tensor_tensor", r1[:], r1[:], cb[1][:], ALU.subtract, r=["r1", "cb1"], w=["r1"])
                          P("dve", "tensor_copy", cb[2][:], r1[:], r=["r1"], w=["cb2"])
                          if tt == 0 and s == 0:
                              P("dve", "memset", auxq[:], 0.0, w=["auxq"])
                          for i3 in range(3):
                              P("dve", "tensor_copy", auxq[32 * i3:32 * i3 + 1, :, s:s + 1], cb[i3][32 * i3:32 * i3 + 1, :].unsqueeze(2),
                                r=["cb%d" % i3, "auxq"], w=["auxq"])
                      nkb = 4 * tt + 4
                      it = 0
                      for h in range(8):
                          pair = h // 2
                          base = 64 * (h % 2)
                          O, ok = accps()
                          for kb in range(nkb):
                              j = kb - 4 * tt
                              c0 = 128 * j if j > 0 else 0
                              sc, sk = newps()
                              P("pe", "matmul", sc[:, c0:512], KT[base:base + 64, pair, 128 * kb:128 * kb + 128], qT[base:base + 64, pair, c0:512],
                                start=True, stop=False, r=[("KT", pair), ("qT", pair)], w=[sk])
                              if j >= 0:
                                  P("pe", "matmul", sc[:, 128 * j:128 * j + 128], ident_bf[:], maskT[:], start=False, stop=False,
                                    r=["ident_bf", "maskT"], w=[sk])
                              P("pe", "matmul", sc[:, c0:512], ones_bf[0:96, :],
                                auxq[0:96, h, c0 // 128:4].unsqueeze(2).to_broadcast([96, 4 - c0 // 128, 128]),
                                start=False, stop=True, r=["ones_bf", "auxq"], w=[sk])
                              pb = it % 3
                              it += 1
                              P("act", "activation", pT[pb][:, c0:512], sc[:, c0:512], AF.Exp, bias=ncum[:, kb, h:h + 1],
                                r=[sk, "ncum"], w=[("pT", pb)])
                              P("pe", "matmul", O[0:65, c0:512], Vc[:, kb, h, :], pT[pb][:, c0:512], start=(kb == 0), stop=(kb == nkb - 1),
                                r=["Vc", ("pT", pb)], w=[ok])
                          P("dve", "reciprocal", rden[64:65, :], O[64:65, :], r=[ok], w=["rden"])
                          P("act", "copy", onum[:], O[0:64, :], r=[ok], w=["onum"])
                          Bc, bk_ = newps()
                          P("pe", "matmul", Bc[0:64, :], ones_f[64:65, 0:64], rden[64:65, :], start=True, stop=True, r=["ones_f", "rden"], w=[bk_])
                          P("dve", "tensor_tensor", yc[:, h, :], onum[:], Bc[0:64, :], ALU.mult, r=["onum", bk_], w=[("yc", h)])
                      S.barrier()
                      stage("attn", [yc[:, h, :] for h in range(8)])

                  with contextlib.ExitStack() as ph:
                      acc = ph.enter_context(sbt("acc", [128, 8, 512], F32))
                      sg = [ph.enter_context(sbt("sg%d" % i, [128, 512], F32)) for i in range(2)]
                      pm = [ph.enter_context(sbt("pm%d" % i, [128, 512], F32)) for i in range(2)]
                      ii = 0
                      for br in range(3):
                          for hf in range(2):
                              gv, gk = inproj_block(3592 + 1024 * br + 512 * hf)
                              if br < 2:
                                  bw, bwk = load_w(prm[("w_branch", l)][512 * br:512 * br + 512, 512 * hf:512 * hf + 512].rearrange("(kc p) n -> p kc n", p=128),
                                                   lambda b: b[:, 0:2048].rearrange("p (kc n) -> p kc n", kc=4))
                                  bwv = bw[:, 0:2048].rearrange("p (kc n) -> p kc n", kc=4)
                              else:
                                  bw, bwk = load_w(prm[("w_branch", l)][1024:1536, 512 * hf:512 * hf + 512].rearrange("(h p) n -> p h n", p=64),
                                                   lambda b: b[0:64, 0:4096].rearrange("p (h n) -> p h n", h=8))
                                  bwv = bw[0:64, 0:4096].rearrange("p (h n) -> p h n", h=8)
                              for m in range(4):
                                  dc = 4 * hf + m
                                  psg, pgk = fm_matmul(gv, gk, m)
                                  sb_ = ii % 2
                                  ii += 1
                                  P("act", "activation", sg[sb_][:], psg[:], AF.Sigmoid, bias=bg[:, 8 * br + dc:8 * br + dc + 1],
                                    r=[pgk, "bg"], w=[("sg", sb_)])
                                  pp, ppk = newps()
                                  if br < 2:
                                      src = ya if br == 0 else yb
                                      sname = "ya" if br == 0 else "yb"
                                      for kc in range(4):
                                          P("pe", "matmul", pp[:], bwv[:, kc, 128 * m:128 * m + 128], src[:, kc, :], start=(kc == 0), stop=(kc == 3),
                                            r=[bwk, (sname, kc)], w=[ppk])
                                  else:
                                      for hh in range(8):
                                          P("pe", "matmul", pp[:], bwv[:, hh, 128 * m:128 * m + 128], yc[:, hh, :], start=(hh == 0), stop=(hh == 7),
                                            r=[bwk, ("yc", hh)], w=[ppk])
                                  if br == 0:
                                      P("dve", "tensor_tensor", acc[:, dc, :], pp[:], sg[sb_][:], ALU.mult, r=[ppk, ("sg", sb_)], w=[("acc", dc)])
                                  else:
                                      P("dve", "tensor_tensor", pm[sb_][:], pp[:], sg[sb_][:], ALU.mult, r=[ppk, ("sg", sb_)], w=[("pm", sb_)])
                                      if br == 1:
                                          P("dve", "tensor_tensor", acc[:, dc, :], acc[:, dc, :], pm[sb_][:], ALU.add,
                                            r=[("acc", dc), ("pm", sb_)], w=[("acc", dc)])
                                      else:
                                          P("dve", "tensor_tensor", mrg[:, dc, :], acc[:, dc, :], pm[sb_][:], ALU.add,
                                            r=[("acc", dc), ("pm", sb_)], w=[("mrg", dc)])
                      S.barrier()
                      stage("merge", [mrg[:, dc, :] for dc in range(8)])

                  def out_proj_postnorm(lhs, lhs_keys, nk, wname, gname, phs):
                      oacc = phs.enter_context(sbt("oacc", [128, 4, D], F32))
                      junk = phs.enter_context(sbt("junk2", [128, D], BF16))
                      ss = phs.enter_context(sbt("ss2", [128, 4], F32))
                      rstd = phs.enter_context(sbt("rstd2", [128, 4], F32))
                      tnn = phs.enter_context(sbt("tnn", [128, D], F32))
                      DMA("sp", gpost[:], prm[(gname, l)].rearrange("(o d) -> o d", o=1).partition_broadcast(128), w=["gpost"], nonc=True)
                      return oacc, junk, ss, rstd, tnn

                  def postnorm(oacc, junk, ss, rstd, tnn, okeys):
                      P("dve", "memset", ss[:], 0.0, w=["ss"])
                      for s in range(4):
                          P("act", "activation", junk[:], oacc[:, s, :], AF.Square, accum_out=ss[:, s:s + 1],
                            r=[okeys(s, 0), okeys(s, 1), "ss"], w=["junk", "ss"])
                      P("act", "activation", rstd[:], ss[:], AF.Sqrt, bias=EPS, scale=1.0 / D, r=["ss"], w=["rstd"])
                      P("dve", "reciprocal", rstd[:], rstd[:], r=["rstd"], w=["rstd"])
                      for s in range(4):
                          P("dve", "tensor_tensor", tnn[:], oacc[:, s, :], gpost[:], ALU.mult, r=[okeys(s, 0), okeys(s, 1), "gpost"], w=["tnn"])
                          P("dve", "scalar_tensor_tensor", xt[:, s, :], tnn[:], rstd[:, s:s + 1], xt[:, s, :], ALU.mult, ALU.add,
                            r=["tnn", "rstd", ("xt", s)], w=[("xt", s)])

                  with contextlib.ExitStack() as ph:
                      oacc, junk, ss, rstd, tnn = out_proj_postnorm(None, None, None, None, "g_post_mix", ph)
                      ok_ = lambda s, hf: ("oacc", s, hf)
                      for hf in range(2):
                          wb, wk = load_w(prm[("w_out", l)][:, 512 * hf:512 * hf + 512].rearrange("(kc p) n -> p kc n", p=128),
                                          lambda b: b[:, 0:4096].rearrange("p (kc n) -> p kc n", kc=8))
                          wv = wb[:, 0:4096].rearrange("p (kc n) -> p kc n", kc=8)
                          for s in range(4):
                              ps, pk = newps()
                              for kc in range(8):
                                  P("pe", "matmul", ps[:], mrg[:, kc, 128 * s:128 * s + 128], wv[:, kc, :], start=(kc == 0), stop=(kc == 7),
                                    r=[wk, ("mrg", kc)], w=[pk])
                              P("act", "copy", oacc[:, s, 512 * hf:512 * hf + 512], ps[:], r=[pk], w=[ok_(s, hf)])
                      postnorm(oacc, junk, ss, rstd, tnn, ok_)
                      S.barrier()
                      stage("mix", [xt[:, s_, :] for s_ in range(4)])

                  mixer.close()
                  norm_T(1, "b")
                  with contextlib.ExitStack() as ph:
                      oacc, junk, ss, rstd, tnn = out_proj_postnorm(None, None, None, None, "g_post_mlp", ph)
                      hid = [ph.enter_context(sbt("hid%d" % i, [128, 4, 512], BF16)) for i in range(2)]
                      rl = [ph.enter_context(sbt("rl%d" % i, [128, 512], F32)) for i in range(2)]
                      ok_ = lambda s, hf: ("oacc", s, hf)
                      ri_ = 0
                      for fb in range(8):
                          hb = fb % 2
                          w1b, w1k = load_w(prm[("w_ff1", l)][:, 512 * fb:512 * fb + 512].rearrange("(kc p) n -> p kc n", p=128),
                                            lambda b: b[:, 0:4096].rearrange("p (kc n) -> p kc n", kc=8))
                          w1v = w1b[:, 0:4096].rearrange("p (kc n) -> p kc n", kc=8)
                          for m in range(4):
                              ps, pk = fm_matmul(w1v, w1k, m)
                              rb_ = ri_ % 2
                              ri_ += 1
                              P("act", "activation", rl[rb_][:], ps[:], AF.Relu, r=[pk], w=[("rl", rb_)])
                              P("dve", "tensor_tensor", hid[hb][:, m, :], rl[rb_][:], rl[rb_][:], ALU.mult, r=[("rl", rb_)], w=[("hid", hb, m)])
                          w2b, w2k = load_w(prm[("w_ff2", l)][512 * fb:512 * fb + 512, :].rearrange("(kc p) n -> p kc n", p=128),
                                            lambda b: b[:, 0:4096].rearrange("p (kc n) -> p kc n", kc=4))
                          w2v = w2b[:, 0:4096].rearrange("p (kc n) -> p kc n", kc=4)
                          for s in range(4):
                              for hf in range(2):
                                  ps, pk = newps()
                                  for kc in range(4):
                                      P("pe", "matmul", ps[:], hid[hb][:, kc, 128 * s:128 * s + 128], w2v[:, kc, 512 * hf:512 * hf + 512],
                                        start=(kc == 0), stop=(kc == 3), r=[w2k, ("hid", hb, kc)], w=[pk])
                                  if fb == 0:
                                      P("act", "copy", oacc[:, s, 512 * hf:512 * hf + 512], ps[:], r=[pk], w=[ok_(s, hf)])
                                  else:
                                      P("dve", "tensor_tensor", oacc[:, s, 512 * hf:512 * hf + 512], oacc[:, s, 512 * hf:512 * hf + 512], ps[:], ALU.add,
                                        r=[pk, ok_(s, hf)], w=[ok_(s, hf)])
                      postnorm(oacc, junk, ss, rstd, tnn, ok_)
                      S.barrier()
                  od = DMA("sp", xdst[T0:T0 + 512, :].rearrange("(s p) d -> p s d", p=128), xt[:], r=xk)
                  S.barrier()
          S.emit([od])
    except _Stop:
        pass
    return nc


_CACHE = {}


def _get_nc(NT, layers):
    key = (NT, tuple(layers))
    if key not in _CACHE:
        _CACHE[key] = build(NT, layers)
    return _CACHE[key]


def _in_map(xb, inputs, layers, idx_map=None):
    m = {"x": np.ascontiguousarray(xb, dtype=np.float32)}
    for li, l in enumerate(sorted(set(layers))):
        for n in PNAMES:
            m["%s_%d" % (n, l)] = np.ascontiguousarray(inputs[n][l if idx_map is None else idx_map[l]], dtype=np.float32)
    return m


FUSED = True


def kernel(**inputs):
    x = np.asarray(inputs["x"], dtype=np.float32)
    B, L, _ = x.shape
    NT = L // 512
    if FUSED:
        nc = _get_nc(NT, (0, 1))
        in_maps = [_in_map(x[b], inputs, (0, 1)) for b in range(B)]
        res = run_bass_kernel_spmd(nc, in_maps, core_ids=list(range(B)))
        return np.stack([np.asarray(r["y"], dtype=np.float32) for r in res.results], axis=0)
    nc = _get_nc(NT, (0,))
    cur = x
    for l in range(DEPTH):
        in_maps = [_in_map(cur[b], inputs, (0,), idx_map={0: l}) for b in range(B)]
        res = run_bass_kernel_spmd(nc, in_maps, core_ids=list(range(B)))
        cur = np.stack([np.asarray(r["y"], dtype=np.float32) for r in res.results], axis=0)
    return cur
```
